# Optimizing a Trainium2 kernel written in Bass

```python
import math
import jax, jax.numpy as jnp
from jax import lax
import numpy as np

D_MODEL = 1024
BATCH = 8
SEQ = 4096
DEPTH = 1

N_META = 16
CONV_WIDTH = 3
D_CONV = D_MODEL
CONV_GROUPS = 16
SB_HEAD_DIM = 64
SB_HEADS = 16
D_ATTN = SB_HEADS * SB_HEAD_DIM
N_BRANCH = 2
D_IN = 3 * D_CONV + 3 * D_ATTN + N_BRANCH * D_MODEL
D_FF = 2816
Q_BLOCK = 128
RMS_EPS = 1e-6

kernel_name = "hybrid_shortconv_stickbreaking_convffn_block"


def rms_norm(x, g):
    xf = x.astype(jnp.float32)
    y = xf * lax.rsqrt(jnp.mean(xf * xf, axis=-1, keepdims=True) + RMS_EPS)
    return (y * g.astype(jnp.float32)).astype(x.dtype)


def causal_dwconv(x, w):
    k, c = w.shape
    return lax.conv_general_dilated(
        x, w.reshape(k, 1, c).astype(x.dtype),
        window_strides=(1,), padding=[(k - 1, 0)],
        dimension_numbers=("NWC", "WIO", "NWC"),
        feature_group_count=c)


def stick_breaking_attention(q, k, v):
    seq_len = q.shape[1]
    scale = 1.0 / math.sqrt(q.shape[-1])
    bounds = [(0, N_META)] + [(s, min(s + Q_BLOCK, seq_len))
                              for s in range(N_META, seq_len, Q_BLOCK)]
    outs = []
    for q0, q1 in bounds:
        qb = q[:, q0:q1]
        kb = k[:, :q1]
        vb = v[:, :q1].astype(jnp.float32)
        z = jnp.einsum("bqhd,bkhd->bhqk", qb, kb,
                       preferred_element_type=jnp.float32) * scale
        t_idx = jnp.arange(q0, q1)[:, None]
        s_idx = jnp.arange(q1)[None, :]
        causal = s_idx < t_idx
        log_beta = jax.nn.log_sigmoid(z)
        log_fail = jnp.where(causal, log_beta - z, 0.0)
        survive = lax.cumsum(log_fail, axis=3, reverse=True) - log_fail
        a = jnp.where(causal, jnp.exp(log_beta + survive), 0.0)
        o = jnp.einsum("bhqk,bkhd->bqhd", a, vb)
        outs.append(o.astype(v.dtype))
    return jnp.concatenate(outs, axis=1)


def hybrid_mixer(xn, w_in, conv_w_mix, w_proj_conv, w_proj_attn, b_gate, w_out):
    bsz, seq_len, _ = xn.shape
    h = xn @ w_in
    splits = np.cumsum([D_CONV, D_CONV, D_CONV, D_ATTN, D_ATTN, D_ATTN, D_MODEL])
    b_c, c_c, h_c, q, k, v, g_conv, g_attn = jnp.split(h, list(splits), axis=-1)

    y_conv = b_c * causal_dwconv(c_c * h_c, conv_w_mix)
    branch_conv = y_conv @ w_proj_conv

    hd = (bsz, seq_len, SB_HEADS, SB_HEAD_DIM)
    o = stick_breaking_attention(q.reshape(hd), k.reshape(hd), v.reshape(hd))
    branch_attn = o.reshape(bsz, seq_len, D_ATTN) @ w_proj_attn

    gate_conv = jax.nn.sigmoid(g_conv + b_gate[0])
    gate_attn = jax.nn.sigmoid(g_attn + b_gate[1])
    merged = gate_conv * branch_conv + gate_attn * branch_attn
    return merged @ w_out


def conv_gated_mlp(xn, w_up_gate, conv_w_ffn, w_down):
    u, g = jnp.split(xn @ w_up_gate, 2, axis=-1)
    u = causal_dwconv(u, conv_w_ffn)
    return (jax.nn.gelu(u) * g) @ w_down


def setup_inputs(seed: int = 0) -> dict:
    key = jax.random.key(seed)
    ks = jax.random.split(key, 16)
    f32 = jnp.float32

    def normal(k, shape, scale):
        return jax.random.normal(k, shape, f32) * scale

    def gain(k):
        return 1.0 + normal(k, (DEPTH, D_MODEL), 0.05)

    return {
        "x": normal(ks[0], (BATCH, SEQ, D_MODEL), 1.0),
        "meta_tokens": normal(ks[1], (N_META, D_MODEL), 1.0),
        "g_pre_mix": gain(ks[2]),
        "w_in": normal(ks[3], (DEPTH, D_MODEL, D_IN), D_MODEL ** -0.5),
        "conv_w_mix": normal(ks[4], (DEPTH, CONV_WIDTH, D_CONV), CONV_WIDTH ** -0.5),
        "w_proj_conv": normal(ks[5], (DEPTH, D_CONV, D_MODEL), D_CONV ** -0.5),
        "w_proj_attn": normal(ks[6], (DEPTH, D_ATTN, D_MODEL), D_ATTN ** -0.5),
        "b_gate": normal(ks[7], (DEPTH, N_BRANCH, D_MODEL), 0.02),
        "w_out": normal(ks[8], (DEPTH, D_MODEL, D_MODEL), D_MODEL ** -0.5),
        "g_post_mix": gain(ks[9]),
        "g_pre_ffn": gain(ks[10]),
        "w_up_gate": normal(ks[11], (DEPTH, D_MODEL, 2 * D_FF), D_MODEL ** -0.5),
        "conv_w_ffn": normal(ks[12], (DEPTH, CONV_WIDTH, D_FF), CONV_WIDTH ** -0.5),
        "w_down": normal(ks[13], (DEPTH, D_FF, D_MODEL), D_FF ** -0.5),
        "g_post_ffn": gain(ks[14]),
    }


def reference(x, meta_tokens, g_pre_mix, w_in, conv_w_mix, w_proj_conv, w_proj_attn,
              b_gate, w_out, g_post_mix, g_pre_ffn, w_up_gate, conv_w_ffn, w_down,
              g_post_ffn):
    bsz = x.shape[0]
    meta = jnp.broadcast_to(meta_tokens[None].astype(x.dtype), (bsz, N_META, D_MODEL))
    h = jnp.concatenate([meta, x], axis=1)
    for layer in range(DEPTH):
        mix = hybrid_mixer(rms_norm(h, g_pre_mix[layer]), w_in[layer], conv_w_mix[layer],
                           w_proj_conv[layer], w_proj_attn[layer], b_gate[layer],
                           w_out[layer])
        h = h + rms_norm(mix, g_post_mix[layer])
        ffn = conv_gated_mlp(rms_norm(h, g_pre_ffn[layer]), w_up_gate[layer],
                             conv_w_ffn[layer], w_down[layer])
        h = h + rms_norm(ffn, g_post_ffn[layer])
    return h[:, N_META:, :]
```

```python
import bisect
import numpy as np
import concourse.bass as bass
import concourse.mybir as mybir
from concourse.bass_utils import run_bass_kernel_spmd

F32 = mybir.dt.float32
BF16 = mybir.dt.bfloat16
AF = mybir.ActivationFunctionType
ALU = mybir.AluOpType
DSIZE = {F32: 4, BF16: 2}

D = 1024
NM = 16
SEQ = 4096
L = NM + SEQ
DFF = 2816
NFF = DFF // 128
NH = 16
HD = 64
TCH = 512
EPS = 1e-6
N_CORES = 8


class Op:
    __slots__ = ("eng", "fn", "deps", "signal", "chan", "ordinal")

    def __init__(self, eng, fn, chan=None):
        self.eng = eng
        self.fn = fn
        self.deps = []
        self.signal = chan is not None
        self.chan = chan
        self.ordinal = (0, 0)

    @property
    def key(self):
        return self.chan if self.chan is not None else self.eng


class Rec:
    __slots__ = ("lo", "hi", "w", "r")

    def __init__(self, lo, hi, w=None, r=None):
        self.lo = lo
        self.hi = hi
        self.w = w
        self.r = r if r is not None else {}


class Prog:
    ENGS = ("PE", "ACT", "DVE", "POOL", "SP")
    LIMIT = 30000

    def __init__(self, nc):
        self.nc = nc
        self.streams = {e: [] for e in self.ENGS}
        self.spaces = {}
        self.base = {}
        self.chans = []
        self.nops = 0
        self.psum = set()

    def reg(self, t, space=None, base=0, psum=False):
        self.base[t.name] = (space if space is not None else t.name, base)
        if psum:
            self.psum.add(t.name)
        return t

    def rng(self, ap):
        if isinstance(ap, tuple):
            return ap
        space, base = self.base[ap.tensor.name]
        ent = ap.ap
        ds = DSIZE[ap.dtype]
        pstep = ent[0][0]
        off = ap.offset
        if pstep > 0:
            off = off % pstep
        lo = off
        hi = off
        for st, cnt in ent[1:]:
            d = st * (cnt - 1)
            if d < 0:
                lo += d
            else:
                hi += d
        return space, base + lo * ds, base + (hi + 1) * ds

    def _cover(self, space, lo, hi):
        lst = self.spaces.setdefault(space, [])
        i = bisect.bisect_right(lst, lo, key=lambda r: r.hi)
        j = bisect.bisect_left(lst, hi, key=lambda r: r.lo)
        seg = []
        for r in lst[i:j]:
            pieces = [r]
            for cut in (lo, hi):
                nxt = []
                for p in pieces:
                    if p.lo < cut < p.hi:
                        nxt.append(Rec(p.lo, cut, p.w, dict(p.r)))
                        nxt.append(Rec(cut, p.hi, p.w, dict(p.r)))
                    else:
                        nxt.append(p)
                pieces = nxt
            seg.extend(pieces)
        full = []
        cur = lo
        for r in seg:
            if r.hi <= lo or r.lo >= hi:
                full.append(r)
                continue
            if r.lo > cur:
                full.append(Rec(cur, r.lo))
            full.append(r)
            cur = r.hi
        if cur < hi:
            full.append(Rec(cur, hi))
        full.sort(key=lambda x: x.lo)
        lst[i:j] = full
        return [r for r in full if r.lo >= lo and r.hi <= hi]

    def emit(self, eng, fn, reads=(), writes=(), chan=None, extra_deps=()):
        op = Op(eng, fn, chan)
        self.nops += 1
        deps = set(extra_deps)
        reads = list(reads)
        writes = list(writes)
        for ap in reads + writes:
            if isinstance(ap, tuple) or ap.tensor.name not in self.psum:
                continue
            sp, lo, hi = self.rng(ap)
            lo = lo // 2048 * 2048
            hi = (hi + 2047) // 2048 * 2048
            for r in self._cover(sp + "#bank", lo, hi):
                for k_, ro in r.r.items():
                    if k_ != eng:
                        deps.add(ro)
                r.r[eng] = op
        for ap in reads:
            sp, lo, hi = self.rng(ap)
            for r in self._cover(sp, lo, hi):
                if r.w is not None:
                    deps.add(r.w)
                r.r[op.key] = op
        for ap in writes:
            sp, lo, hi = self.rng(ap)
            for r in self._cover(sp, lo, hi):
                if r.w is not None:
                    deps.add(r.w)
                for ro in r.r.values():
                    if ro is not op:
                        deps.add(ro)
                r.w = op
                r.r = {}
        for d in deps:
            if d is op:
                continue
            if d.chan is None and d.eng == eng and eng == "PE":
                continue
            d.signal = True
            op.deps.append(d)
        if chan is not None and chan not in self.chans:
            self.chans.append(chan)
        self.streams[eng].append(op)
        return op

    def finalize(self):
        nc = self.nc
        keys = list(self.ENGS) + self.chans
        sems = {}

        def sem_for(k, ep):
            if (k, ep) not in sems:
                sems[(k, ep)] = nc.alloc_semaphore(name="s_%s_%d" % (k, ep))
            return sems[(k, ep)]

        state = {k: [0, 0] for k in keys}
        finals = {}
        for e in self.ENGS:
            for op in self.streams[e]:
                if op.signal:
                    k = op.key
                    inc = 16 if op.chan is not None else 1
                    ep, c = state[k]
                    if c + inc > self.LIMIT:
                        ep += 1
                        c = 0
                    c += inc
                    state[k] = [ep, c]
                    op.ordinal = (ep, c)
                    if op.chan is not None:
                        finals[(k, ep)] = c
        with nc.Block() as block:
            def run(e, name):
                waited = {}
                for op in self.streams[name]:
                    for d in op.deps:
                        k = d.key
                        ep, c = d.ordinal
                        w = waited.get(k)
                        if w is not None and (w[0] > ep or (w[0] == ep and w[1] >= c)):
                            continue
                        e.wait_ge(sem_for(k, ep), c)
                        waited[k] = (ep, c)
                    ins = op.fn(e)
                    if op.signal:
                        ins.then_inc(sem_for(op.key, op.ordinal[0]), 16 if op.chan is not None else 1)
                if name == "SP":
                    for (k, ep), c in finals.items():
                        e.wait_ge(sem_for(k, ep), c)

            @block.tensor
            def _(e):
                run(e, "PE")

            @block.scalar
            def _(e):
                run(e, "ACT")

            @block.vector
            def _(e):
                run(e, "DVE")

            @block.gpsimd
            def _(e):
                run(e, "POOL")

            @block.sync
            def _(e):
                run(e, "SP")
        return state


CST_COLS = 512 + 896 + 512
NVEC = 8 + 8 + 24 + 66 + 16


class _Stop(Exception):
    pass


STOP_STAGE = [None]
DBG_HEADS = list(range(16))


def build_program(n_chunks_limit=None):
    nc = bass.Bass("TRN2", target_bir_lowering=False)
    P = Prog(nc)

    def din(name, shape, dt=F32):
        return nc.dram_tensor(name, shape, dt, kind="ExternalInput").ap()

    x_d = din("x", [SEQ, D])
    meta_d = din("meta", [NM, D])
    cst_d = din("cst", [128, CST_COLS])
    vec_d = din("vecs", [128, NVEC])
    gpost_d = din("gpost", [2, 128, D])
    w_in_d = din("w_in", [D, 8 * D])
    w_pc_d = din("w_pc", [D, D])
    w_pa_d = din("w_pa", [D, D])
    w_out_d = din("w_out", [D, D])
    w_ug_d = din("w_ug", [D, 2 * DFF])
    w_dn_d = din("w_dn", [DFF, D])
    out_d = nc.dram_tensor("out", [SEQ, D], F32, kind="ExternalOutput").ap()

    win_s = nc.dram_tensor("win_s", [D, 8 * D], BF16, kind="Internal").ap()
    wmix_s = nc.dram_tensor("wmix_s", [D, 4 * D], BF16, kind="Internal").ap()
    wout_s = nc.dram_tensor("wout_s", [D, D], BF16, kind="Internal").ap()
    wug_s = nc.dram_tensor("wug_s", [D, 2 * DFF], BF16, kind="Internal").ap()
    wdn_s = nc.dram_tensor("wdn_s", [DFF, D], BF16, kind="Internal").ap()

    SB0 = 16512
    cursor = [SB0]

    def sb(name, shape, dt, off=None, space=None):
        n = DSIZE[dt]
        for s_ in shape[1:]:
            n *= s_
        if off is None:
            a = cursor[0]
            cursor[0] += (n + 31) // 32 * 32
            assert cursor[0] <= 229344, ("sbuf overflow", name, cursor[0])
            t = nc.alloc_sbuf_tensor_at(name, shape, dt, offset=a)
            P.reg(t, "SB", a)
        else:
            t = nc.alloc_sbuf_tensor_at(name, shape, dt, offset=off)
            P.reg(t, "SB", off)
        return t

    Kc = sb("Kc", [128, 8, L], BF16)
    dVc = sb("dVc", [128, 33, D], BF16)
    cstb = sb("cstb", [128, CST_COLS], BF16)
    vecs = sb("vecs_sb", [128, NVEC], F32)
    vprev = sb("vprev", [128, D], BF16)
    uhm = sb("uhm", [128, 8, 2], F32)
    uhf = sb("uhf", [128, NFF, 2], F32)
    stat = sb("stat", [128, 96], F32)
    Wb = [sb("W0", [128, 8, 512], BF16), sb("W1", [128, 8, 512], BF16)]
    A0 = cursor[0]
    ARENA = 55296
    assert A0 + ARENA <= 229344, (A0, ARENA)
    KB = 1024

    def ar(name, shape, dt, off):
        n = DSIZE[dt]
        for s_ in shape[1:]:
            n *= s_
        assert off + n <= ARENA, (name, off, n)
        return sb(name, shape, dt, off=A0 + off)

    xntok = ar("xntok", [128, 4, D], BF16, 0)
    ycT = ar("ycT", [128, 8, TCH], BF16, 0)
    xn2tok = ar("xn2tok", [128, 4, D], BF16, 0)
    actT = ar("actT", [128, NFF, TCH], BF16, 0)
    Vtok = ar("Vtok", [128, 4, D], BF16, 8192)
    mergedT = ar("mergedT", [128, 8, TCH], BF16, 8192)
    Fb = [ar("F%d" % i, [128, 512], F32, 16384 + i * 2048) for i in range(3)]
    Otok = ar("Otok", [128, D], BF16, 16384)
    junk = ar("junk", [128, D], BF16, 20480)
    gbc1 = ar("gbc1", [128, D], F32, 16384)
    xnT = ar("xnT", [128, 8, TCH], BF16, 22528)
    xn2T = ar("xn2T", [128, 8, TCH], BF16, 22528)
    tmp4 = ar("tmp4", [128, D], F32, 22528)
    gbc2 = ar("gbc2", [128, D], F32, 26624)
    xin = ar("xin", [128, 4, D], F32, 30720)
    OT = ar("OT", [128, 8, TCH], BF16, 30720)
    QT = ar("QT", [128, 8, TCH], BF16, 38912)
    Ares = ar("Ares", [128, 4, D], F32, 30720)
    Pb = [ar("P%d" % i, [128, 512], BF16, 47104 + i * KB) for i in range(4)]
    PTb = [ar("PT%d" % i, [128, 512], BF16, 51200 + i * KB) for i in range(2)]
    ub = ar("ub", [128, 516], F32, 38912)
    hs = ar("hs", [128, 512], F32, 38912 + 2112)
    acc = ar("acc", [128, 512], F32, 38912 + 2112 + 2048)
    sg1 = ar("sg1", [128, 512], F32, 38912 + 2112 + 4096)
    sg2 = ar("sg2", [128, 512], F32, 38912 + 2112 + 6144)
    m1 = ar("m1", [128, 512], F32, 38912 + 2112)
    ubf = ar("ubf", [128, 516], F32, 47104)
    accf = ar("accf", [128, 512], F32, 47104 + 2080)
    gef = ar("gef", [128, 512], F32, 47104 + 2080 + 2048)
    NST = 4
    ST32 = [sb("ST32_%d" % i, [128, 3072], F32, off=SB0 + i * 12 * KB) for i in range(NST)]
    STB = [sb("STB_%d" % i, [128, 3072], BF16, off=SB0 + 65792 + i * 6 * KB) for i in range(NST)]
    cst32 = ar("cst32", [128, CST_COLS], F32, 36 * KB)

    psA = P.reg(nc.alloc_psum_tensor("psA", [128, 8, 512], F32), psum=True)
    psG = psA[:, 0:2, :]
    psZ = psA[:, 2:4, :]
    psO = psA[:, 4:6, :].rearrange("p a b -> p (a b)")
    psT = psA[:, 6:8, :].bitcast(BF16)

    def gb(i, r0, r1, c0, c1):
        i = i % 4
        return psG[r0:r1, i, c0:c1] if i < 2 else psZ[r0:r1, i - 2, c0:c1]

    def bk6(i, r0, r1, c0, c1):
        i = i % 6
        if i < 2:
            return psG[r0:r1, i, c0:c1]
        if i < 4:
            return psZ[r0:r1, i - 2, c0:c1]
        return psO[r0:r1, (i - 4) * 512 + c0:(i - 4) * 512 + c1]

    rot = [0]

    ident = cstb[:, 0:128]
    Dmat = cstb[:, 128:256]
    Emat = cstb[:, 256:384]
    E16 = cstb[:, 384:512]
    IND0 = 512
    ZER0 = 512 + 896

    def v_gpm(ft):
        return vecs[:, ft:ft + 1]

    def v_gpf(ft):
        return vecs[:, 8 + ft:9 + ft]

    def v_cwm(ft, i):
        return vecs[:, 16 + ft * 3 + i:17 + ft * 3 + i]

    def v_cwf(ft, i):
        return vecs[:, 40 + ft * 3 + i:41 + ft * 3 + i]

    def v_bg(br, ft):
        return vecs[:, 106 + br * 8 + ft:107 + br * 8 + ft]

    def mm(out, lhsT, rhs, start, stop):
        P.emit("PE", lambda e, o=out, l=lhsT, r=rhs, s=start, t=stop: e.matmul(o, lhsT=l, rhs=r, start=s, stop=t),
               reads=[lhsT, rhs], writes=[out])

    def tr(out, in_, idn):
        P.emit("PE", lambda e, o=out, i=in_, d=idn: e.transpose(o, i, d), reads=[in_, idn], writes=[out])

    def act(out, in_, func, bias=None, scale=1.0, accum=None, extra_reads=()):
        rd = [in_] + list(extra_reads)
        wr = [out]
        kw = {}
        if bias is not None:
            kw["bias"] = bias
            if not isinstance(bias, float):
                rd.append(bias)
        if accum is not None:
            kw["accum_out"] = accum
            wr.append(accum)
        if not isinstance(scale, float):
            rd.append(scale)
        P.emit("ACT", lambda e, o=out, i=in_, f=func, s=scale, k=kw: e.activation(out=o, in_=i, func=f, scale=s, **k),
               reads=rd, writes=wr)

    def ts(eng, out, in0, s1, s2, op0, op1=None):
        rd = [in0]
        if not isinstance(s1, float):
            rd.append(s1)
        if s2 is not None and not isinstance(s2, float):
            rd.append(s2)
        if op1 is None:
            P.emit(eng, lambda e, o=out, i=in0, a=s1, p0=op0: e.tensor_scalar(o, i, a, None, op0=p0), reads=rd, writes=[out])
        else:
            P.emit(eng, lambda e, o=out, i=in0, a=s1, b=s2, p0=op0, p1=op1: e.tensor_scalar(o, i, a, b, op0=p0, op1=p1),
                   reads=rd, writes=[out])

    def tt(eng, out, in0, in1, op):
        P.emit(eng, lambda e, o=out, a=in0, b=in1, p=op: e.tensor_tensor(out=o, in0=a, in1=b, op=p), reads=[in0, in1], writes=[out])

    def stt(eng, out, in0, scalar, in1, op0, op1):
        rd = [in0, in1]
        if not isinstance(scalar, float):
            rd.append(scalar)
        P.emit(eng, lambda e, o=out, a=in0, s=scalar, b=in1, p0=op0, p1=op1: e.scalar_tensor_tensor(out=o, in0=a, scalar=s, in1=b, op0=p0, op1=p1),
               reads=rd, writes=[out])

    def cp(eng, out, in_):
        P.emit(eng, lambda e, o=out, i=in_: e.tensor_copy(o, i), reads=[in_], writes=[out])

    def memset(eng, out, val):
        P.emit(eng, lambda e, o=out, v=val: e.memset(o, v), writes=[out])

    def dma(out, in_, chan, reads=(), writes=(), extra=()):
        return P.emit("SP", lambda e, o=out, i=in_: e.dma_start(out=o, in_=i), reads=reads, writes=writes, chan=chan, extra_deps=extra)

    dma(cst32[:, :], cst_d, "ldc0", writes=[cst32[:, :]])
    dma(vecs[:, :], vec_d, "ldc1", writes=[vecs[:, :]])
    cp("DVE", cstb[:, :], cst32[:, :])
    memset("DVE", uhm[:, :, :], 0.0)
    memset("DVE", uhf[:, :, :], 0.0)

    pieces = []

    for rg in range(8):
        r0 = rg * 128
        pieces.append(dict(loads=[(0, 3072, w_in_d[r0:r0 + 128, 0:3072])], n=3072, perm=(3, 8), dst=win_s[r0:r0 + 128, 0:3072], name="win_s"))
        pieces.append(dict(loads=[(0, 2560, w_in_d[r0:r0 + 128, 3072:5632])], n=2560, perm=None, dst=win_s[r0:r0 + 128, 3072:5632], name="win_s"))
        pieces.append(dict(loads=[(0, 2560, w_in_d[r0:r0 + 128, 5632:8192])], n=2560, perm=None, dst=win_s[r0:r0 + 128, 5632:8192], name="win_s"))
    for rg in range(8):
        r0 = rg * 128
        pieces.append(dict(loads=[(0, 1024, w_pc_d[r0:r0 + 128, :]), (1024, 1024, w_in_d[r0:r0 + 128, 6144:7168]),
                                  (2048, 1024, w_pa_d[r0:r0 + 128, :]), (3072, 1024, w_in_d[r0:r0 + 128, 7168:8192])],
                           n=4096, perm=(4, 8), dst=wmix_s[r0:r0 + 128, :], name="wmix_s", big=True))
        pieces.append(dict(loads=[(0, 1024, w_out_d[r0:r0 + 128, :])], n=1024, perm=None, dst=wout_s[r0:r0 + 128, :], name="wout_s"))
    for rg in range(8):
        r0 = rg * 128
        for hf in range(2):
            c0 = hf * 1408
            pieces.append(dict(loads=[(0, 1408, w_ug_d[r0:r0 + 128, c0:c0 + 1408]), (1408, 1408, w_ug_d[r0:r0 + 128, DFF + c0:DFF + c0 + 1408])],
                               n=2816, perm=(2, 11), dst=wug_s[r0:r0 + 128, hf * 2816:(hf + 1) * 2816], name="wug_s"))
    for rg in range(NFF):
        r0 = rg * 128
        pieces.append(dict(loads=[(0, 1024, w_dn_d[r0:r0 + 128, :])], n=1024, perm=None, dst=wdn_s[r0:r0 + 128, :], name="wdn_s"))

    scratch_count = {}
    cast_engs = ["ACT", "DVE", "POOL"]
    flat = []
    for pc in pieces:
        if pc["n"] > 3072:
            T_, F_ = pc["perm"]
            for hf in range(2):
                loads = [(t * 512, 512, src[:, hf * 512:(hf + 1) * 512]) for t, (_, _, src) in enumerate(pc["loads"])]
                flat.append(dict(loads=loads, n=2048, perm=(T_, F_ // 2), dst=pc["dst"][:, hf * 2048:(hf + 1) * 2048], name=pc["name"]))
        else:
            flat.append(pc)

    def prep_load(i):
        sp_ = flat[i]
        b = i % NST
        for (o, w, src) in sp_["loads"]:
            dma(ST32[b][:, o:o + w], src, "pl%d" % b, writes=[ST32[b][:, o:o + w]])

    def prep_cast_store(i):
        sp_ = flat[i]
        b = i % NST
        n = sp_["n"]
        eng = cast_engs[i % 3]
        if sp_["perm"] is None:
            cin = ST32[b][:, 0:n]
            cout = STB[b][:, 0:n]
        else:
            T_, F_ = sp_["perm"]
            cin = ST32[b][:, 0:n].rearrange("p (t f c) -> p f t c", t=T_, f=F_, c=128)
            cout = STB[b][:, 0:n].rearrange("p (f t c) -> p f t c", f=F_, t=T_, c=128)
        if eng == "ACT":
            P.emit("ACT", lambda e, o=cout, i_=cin: e.activation(out=o, in_=i_, func=AF.Copy), reads=[ST32[b][:, 0:n]], writes=[STB[b][:, 0:n]])
        else:
            P.emit(eng, lambda e, o=cout, i_=cin: e.tensor_copy(o, i_), reads=[ST32[b][:, 0:n]], writes=[STB[b][:, 0:n]])
        k = scratch_count.get(sp_["name"], 0)
        scratch_count[sp_["name"]] = k + 1
        dma(sp_["dst"], STB[b][:, 0:n], "ps%d" % b, reads=[STB[b][:, 0:n]], writes=[(sp_["name"], k, k + 1)])

    LOOKAHEAD = NST - 1
    for i in range(min(LOOKAHEAD, len(flat))):
        prep_load(i)
    for i in range(len(flat)):
        if i + LOOKAHEAD < len(flat):
            prep_load(i + LOOKAHEAD)
        prep_cast_store(i)

    def scr_read(name):
        return (name, 0, scratch_count[name])

    wstate = {"n": 0}

    def wtile(src_ap, nk, ncols, name):
        b = wstate["n"] % 2
        wstate["n"] += 1
        dst = Wb[b][:, 0:nk, 0:ncols]
        dma(dst, src_ap.rearrange("(k p) c -> p k c", p=128), "w%d" % b, reads=[scr_read(name)], writes=[dst])
        return Wb[b]

    def rms_rstd(src_list, nrows, col0):
        n = len(src_list)
        memset("DVE", stat[0:nrows, col0:col0 + n], 0.0)
        for i, s_ in enumerate(src_list):
            act(junk[0:nrows, :], s_, AF.Square, accum=stat[0:nrows, col0 + i:col0 + i + 1])
        ts("DVE", stat[0:nrows, col0 + 8:col0 + 8 + n], stat[0:nrows, col0:col0 + n], 1.0 / D, EPS, ALU.mult, ALU.add)
        act(stat[0:nrows, col0 + 8:col0 + 8 + n], stat[0:nrows, col0 + 8:col0 + 8 + n], AF.Ln)
        act(stat[0:nrows, col0 + 16:col0 + 16 + n], stat[0:nrows, col0 + 8:col0 + 8 + n], AF.Exp, scale=-0.5)
        return [stat[0:nrows, col0 + 16 + i:col0 + 17 + i] for i in range(n)]

    def transpose_to_feature_major(src_tok, dstT, nb, BT, gain_fn):
        T = nb * BT
        for ft in range(8):
            par = ft % 2
            for j in range(nb):
                tr(psT[:, par, j * BT:(j + 1) * BT], src_tok[0:BT, j, ft * 128:(ft + 1) * 128], ident[0:BT, 0:BT])
            if gain_fn is None:
                cp("DVE", dstT[:, ft, 0:T], psT[:, par, 0:T])
            else:
                ts("DVE", dstT[:, ft, 0:T], psT[:, par, 0:T], gain_fn(ft), None, ALU.mult)

    def chunk(c):
        if c == 0:
            pos0, T, nb, BT = 0, NM, 1, NM
        else:
            pos0, T, nb, BT = NM + TCH * (c - 1), TCH, 4, 128
        blk0 = 0 if c == 0 else 1 + 4 * (c - 1)

        def xrows(dst):
            if c == 0:
                dma(dst[0:NM, 0, :], meta_d, "ldx", writes=[dst[0:NM, 0, :]])
            else:
                r0 = pos0 - NM
                dma(dst[:, :, :], x_d[r0:r0 + T, :].rearrange("(j p) d -> p j d", p=128), "ldx", writes=[dst[:, :, :]])

        xrows(xin)
        rs = rms_rstd([xin[0:BT, j, :] for j in range(nb)], BT, 0)
        for j in range(nb):
            ts("DVE", xntok[0:BT, j, :], xin[0:BT, j, :], rs[j], None, ALU.mult)
        transpose_to_feature_major(xntok, xnT, nb, BT, v_gpm)

        if STOP_STAGE[0] == 3:
            raise _Stop()
        gi = 0
        for half in range(2):
            w = wtile(win_s[:, 4096 + half * 512:4096 + (half + 1) * 512], 8, 512, "win_s")
            for f4 in range(4):
                pr = half * 4 + f4
                pz = gb(gi, 0, 128, 0, T)
                gi += 1
                for kc in range(8):
                    mm(pz, w[:, kc, f4 * 128:(f4 + 1) * 128], xnT[:, kc, 0:T], kc == 0, kc == 7)
                act(Kc[:, pr, pos0:pos0 + T], pz, AF.Copy)
        for half in range(2):
            w = wtile(win_s[:, 5120 + half * 512:5120 + (half + 1) * 512], 8, 512, "win_s")
            for j in range(nb):
                pz = gb(gi, 0, BT, 0, 512)
                gi += 1
                for kc in range(8):
                    mm(pz, xnT[:, kc, j * BT:(j + 1) * BT], w[:, kc, :], kc == 0, kc == 7)
                cp("DVE", Vtok[0:BT, j, half * 512:(half + 1) * 512], pz)
        for j in range(nb):
            for half in range(2):
                pz = gb(gi, 0, BT, 0, 512)
                gi += 1
                hs_ = slice(half * 512, (half + 1) * 512)
                has_prev = not (c == 0)
                mm(pz, Dmat[0:BT, 0:BT], Vtok[0:BT, j, hs_], True, not has_prev)
                if has_prev:
                    if j > 0:
                        mm(pz, Emat[0:128, 0:BT], Vtok[0:128, j - 1, hs_], False, True)
                    elif c == 1:
                        mm(pz, E16[0:NM, 0:BT], vprev[0:NM, hs_], False, True)
                    else:
                        mm(pz, Emat[0:128, 0:BT], vprev[0:128, hs_], False, True)
                cp("DVE", dVc[0:BT, blk0 + j, hs_], pz)
        for half in range(2):
            w = wtile(win_s[:, 3072 + half * 512:3072 + (half + 1) * 512], 8, 512, "win_s")
            for f4 in range(4):
                pr = half * 4 + f4
                pz = gb(gi, 0, 128, 0, T)
                gi += 1
                for kc in range(8):
                    mm(pz, w[:, kc, f4 * 128:(f4 + 1) * 128], xnT[:, kc, 0:T], kc == 0, kc == 7)
                act(QT[:, pr, 0:T], pz, AF.Identity, scale=0.125)

        if STOP_STAGE[0] == 4:
            raise _Stop()
        for j in range(nb):
            q0 = j * BT
            kchunks = [(pos0, BT * (j + 1), True)]
            for cc in range(c - 1, 0, -1):
                kchunks.append((NM + TCH * (cc - 1), TCH, False))
            if c >= 1:
                kchunks.append((0, NM, False))
            nk_ = len(kchunks)
            tiles = []
            for hh in range(8):
                hb = 8 + (hh ^ 1)
                for ki in range(nk_):
                    tiles.append((hh, ki))
                    tiles.append((hb, ki))
            N = len(tiles)
            for half in range(2):
                mm(psO[0:BT, half * 512:(half + 1) * 512], ident[0:BT, 0:BT], Vtok[0:BT, j, half * 512:(half + 1) * 512], True, False)

            def zb(n, Wk):
                return gb(n, 0, BT, 0, Wk)

            def st_qk(n):
                h, ki = tiles[n]
                k0, Wk, dg = kchunks[ki]
                pr, hp = h // 2, (h % 2) * 64
                mm(zb(n, Wk), QT[hp:hp + 64, pr, q0:q0 + BT], Kc[hp:hp + 64, pr, k0:k0 + Wk], True, True)

            def st_sig(n):
                h, ki = tiles[n]
                k0, Wk, dg = kchunks[ki]
                act(Fb[n % 3][0:BT, 0:Wk], zb(n, Wk), AF.Sigmoid, scale=-1.0)

            def st_scan(n):
                h, ki = tiles[n]
                k0, Wk, dg = kchunks[ki]
                fb = Fb[n % 3]
                pb = Pb[n % 4]
                if dg:
                    i0 = IND0 + 384 - 128 * j
                    d1 = cstb[0:BT, i0:i0 + Wk]
                    init = 0.0
                    rd = [fb[0:BT, 0:Wk], d1]
                else:
                    d1 = cstb[0:BT, ZER0:ZER0 + Wk]
                    init = Pb[(n - 2) % 4][0:BT, 0:1]
                    rd = [fb[0:BT, 0:Wk], d1, init]
                P.emit("DVE", lambda e, o=pb[0:BT, Wk - 1::-1] if Wk < 512 else pb[0:BT, ::-1],
                       a=fb[0:BT, Wk - 1::-1] if Wk < 512 else fb[0:BT, ::-1],
                       b=d1[:, ::-1], i=init: e.tensor_tensor_scan(out=o, data0=a, data1=b, initial=i, op0=ALU.mult, op1=ALU.add),
                       reads=rd, writes=[pb[0:BT, 0:Wk]])

            def subblocks(Wk):
                return [(s0, min(128, Wk - s0)) for s0 in range(0, Wk, 128)]

            def st_tr(n):
                h, ki = tiles[n]
                k0, Wk, dg = kchunks[ki]
                pb = Pb[n % 4]
                for sbi, (s0, ws) in enumerate(subblocks(Wk)):
                    tr(psT[0:ws, n % 2, sbi * 128:sbi * 128 + BT], pb[0:BT, s0:s0 + ws], ident[0:BT, 0:BT])

            def st_ev(n):
                h, ki = tiles[n]
                k0, Wk, dg = kchunks[ki]
                nsb = len(subblocks(Wk))
                ws_max = min(128, Wk)
                src = psT[0:ws_max, n % 2, 0:(nsb - 1) * 128 + BT]
                dst = PTb[n % 2][0:ws_max, 0:(nsb - 1) * 128 + BT]
                act(dst, src, AF.Copy)

            def st_pv(n):
                h, ki = tiles[n]
                k0, Wk, dg = kchunks[ki]
                sbs = subblocks(Wk)
                for sbi, (s0, ws) in enumerate(sbs):
                    pos = k0 + s0
                    blk = 0 if pos < NM else 1 + (pos - NM) // 128
                    last = (n >= N - 2 and sbi == len(sbs) - 1)
                    mm(psO[0:BT, h * HD:(h + 1) * HD], PTb[n % 2][0:ws, sbi * 128:sbi * 128 + BT], dVc[0:ws, blk, h * HD:(h + 1) * HD], False, last)

            NP_ = N // 2
            for m in range(NP_ + 3):
                for n in (2 * m, 2 * m + 1):
                    if n < N:
                        st_qk(n)
                for n in (2 * m - 2, 2 * m - 1):
                    if 0 <= n < N:
                        st_sig(n)
                for n in (2 * m - 2, 2 * m - 1):
                    if 0 <= n < N:
                        st_scan(n)
                for n in (2 * m - 6, 2 * m - 5):
                    if 0 <= n < N:
                        st_pv(n)
                for n in (2 * m - 4, 2 * m - 3):
                    if 0 <= n < N:
                        st_tr(n)
                        st_ev(n)
            act(Otok[0:BT, :], psO[0:BT, :], AF.Copy)
            for g4 in range(2):
                for f4 in range(4):
                    ft = g4 * 4 + f4
                    tr(psT[:, g4, f4 * 128:f4 * 128 + BT], Otok[0:BT, ft * 128:(ft + 1) * 128], ident[0:BT, 0:BT])
                act(OT[:, g4 * 4:g4 * 4 + 4, q0:q0 + BT], psT[:, g4, 0:512].rearrange("p (f t) -> p f t", f=4)[:, :, 0:BT], AF.Copy)
        cp("DVE", vprev[0:BT, :], Vtok[0:BT, nb - 1, :])

        if STOP_STAGE[0] == 5:
            raise _Stop()
        for ft in range(8):
            w = wtile(win_s[:, ft * 384:(ft + 1) * 384], 8, 384, "win_s")
            pb_, pc_, ph_ = [bk6(rot[0] + i_, 0, 128, 0, T) for i_ in range(3)]
            rot[0] += 3
            for oi, pz in enumerate((pb_, pc_, ph_)):
                for kc in range(8):
                    mm(pz, w[:, kc, oi * 128:(oi + 1) * 128], xnT[:, kc, 0:T], kc == 0, kc == 7)
            act(hs[:, 0:T], ph_, AF.Copy)
            cp("DVE", ub[:, 0:2], uhm[:, ft, :])
            tt("DVE", ub[:, 2:2 + T], pc_, hs[:, 0:T], ALU.mult)
            cp("DVE", uhm[:, ft, :], ub[:, T:T + 2])
            ts("DVE", acc[:, 0:T], ub[:, 2:2 + T], v_cwm(ft, 2), None, ALU.mult)
            stt("DVE", acc[:, 0:T], ub[:, 1:1 + T], v_cwm(ft, 1), acc[:, 0:T], ALU.mult, ALU.add)
            stt("DVE", acc[:, 0:T], ub[:, 0:T], v_cwm(ft, 0), acc[:, 0:T], ALU.mult, ALU.add)
            tt("DVE", ycT[:, ft, 0:T], pb_, acc[:, 0:T], ALU.mult)

        if STOP_STAGE[0] == 6:
            raise _Stop()
        for ft in range(8):
            w = wtile(wmix_s[:, ft * 512:(ft + 1) * 512], 8, 512, "wmix_s")
            p_bc, p_gc, p_ba, p_ga = [bk6(rot[0] + i_, 0, 128, 0, T) for i_ in range(4)]
            rot[0] += 4
            for oi, (pz, rhsT) in enumerate(((p_bc, ycT), (p_gc, xnT), (p_ba, OT), (p_ga, xnT))):
                for kc in range(8):
                    mm(pz, w[:, kc, oi * 128:(oi + 1) * 128], rhsT[:, kc, 0:T], kc == 0, kc == 7)
            act(sg1[:, 0:T], p_gc, AF.Sigmoid, bias=v_bg(0, ft))
            act(sg2[:, 0:T], p_ga, AF.Sigmoid, bias=v_bg(1, ft))
            tt("DVE", m1[:, 0:T], p_bc, sg1[:, 0:T], ALU.mult)
            tt("DVE", sg2[:, 0:T], p_ba, sg2[:, 0:T], ALU.mult)
            tt("DVE", mergedT[:, ft, 0:T], m1[:, 0:T], sg2[:, 0:T], ALU.add)

        if STOP_STAGE[0] == 7:
            raise _Stop()
        xrows(Ares)
        dma(gbc1[:, :], gpost_d[0], "ldg1", writes=[gbc1[:, :]])
        wo = [wtile(wout_s[:, half * 512:(half + 1) * 512], 8, 512, "wout_s") for half in range(2)]
        pms = [psA[0:BT, 2 * j:2 * j + 2, :].rearrange("p a b -> p (a b)") for j in range(nb)]
        for j in range(nb):
            for half in range(2):
                for kc in range(8):
                    mm(pms[j][:, half * 512:(half + 1) * 512], mergedT[:, kc, j * BT:(j + 1) * BT], wo[half][:, kc, :], kc == 0, kc == 7)
        r1s = rms_rstd(pms, BT, 24)
        for j in range(nb):
            stt("DVE", tmp4[0:BT, :], pms[j], r1s[j], gbc1[0:BT, :], ALU.mult, ALU.mult)
            tt("DVE", Ares[0:BT, j, :], tmp4[0:BT, :], Ares[0:BT, j, :], ALU.add)
        r2s = rms_rstd([Ares[0:BT, j, :] for j in range(nb)], BT, 28)
        for j in range(nb):
            ts("DVE", xn2tok[0:BT, j, :], Ares[0:BT, j, :], r2s[j], None, ALU.mult)
        transpose_to_feature_major(xn2tok, xn2T, nb, BT, v_gpf)

        if STOP_STAGE[0] == 8:
            raise _Stop()
        Th = T
        for ti in range(NFF // 2):
            w = wtile(wug_s[:, ti * 512:(ti + 1) * 512], 8, 512, "wug_s")
            for fi in range(2):
                ft = ti * 2 + fi
                pu, pg = gb(rot[0], 0, 128, 0, Th), gb(rot[0] + 1, 0, 128, 0, Th)
                rot[0] += 2
                for oi, pz in enumerate((pu, pg)):
                    for kc in range(8):
                        mm(pz, w[:, kc, (fi * 2 + oi) * 128:(fi * 2 + oi + 1) * 128], xn2T[:, kc, 0:Th], kc == 0, kc == 7)
                cp("DVE", ubf[:, 0:2], uhf[:, ft, :])
                act(ubf[:, 2:2 + Th], pu, AF.Copy)
                cp("DVE", uhf[:, ft, :], ubf[:, Th:Th + 2])
                act(accf[:, 0:Th], pu, AF.Identity, scale=v_cwf(ft, 2))
                stt("DVE", accf[:, 0:Th], ubf[:, 1:1 + Th], v_cwf(ft, 1), accf[:, 0:Th], ALU.mult, ALU.add)
                stt("DVE", accf[:, 0:Th], ubf[:, 0:Th], v_cwf(ft, 0), accf[:, 0:Th], ALU.mult, ALU.add)
                act(gef[:, 0:Th], accf[:, 0:Th], AF.Gelu_apprx_tanh)
                tt("DVE", actT[:, ft, 0:Th], pg, gef[:, 0:Th], ALU.mult)
        if c == 0:
            return
        dma(gbc2[:, :], gpost_d[1], "ldg2", writes=[gbc2[:, :]])
        for kg in range(3):
            nk = 8 if kg < 2 else NFF - 16
            for colh in range(2):
                w = wtile(wdn_s[kg * 1024:kg * 1024 + nk * 128, colh * 512:(colh + 1) * 512], nk, 512, "wdn_s")
                for jj in range(nb):
                    po = psA[0:BT, 2 * jj + colh, :]
                    for kk in range(nk):
                        mm(po, actT[:, kg * 8 + kk, jj * BT:(jj + 1) * BT], w[:, kk, :], kg == 0 and kk == 0, kg == 2 and kk == nk - 1)
        pds = [psA[0:BT, 2 * jj:2 * jj + 2, :].rearrange("p a b -> p (a b)") for jj in range(nb)]
        r3s = rms_rstd(pds, BT, 48)
        for jj in range(nb):
            pm = pds[jj]
            r3 = r3s[jj]
            stt("DVE", tmp4[0:BT, :], pm, r3, gbc2[0:BT, :], ALU.mult, ALU.mult)
            tt("DVE", tmp4[0:BT, :], tmp4[0:BT, :], Ares[0:BT, jj, :], ALU.add)
            r0 = pos0 - NM + jj * 128
            dma(out_d[r0:r0 + 128, :], tmp4[0:BT, :], "sto", reads=[tmp4[0:BT, :]])

    nch = 9 if n_chunks_limit is None else n_chunks_limit
    try:
        for c in range(nch):
            chunk(c)
    except _Stop:
        pass
    P.finalize()
    return nc, P


_CACHE = {}


def _consts():
    c = np.zeros((128, CST_COLS), np.float32)
    c[:, 0:128] = np.eye(128, dtype=np.float32)
    dm = -np.eye(128, dtype=np.float32)
    for s in range(1, 128):
        dm[s - 1, s] = 1.0
    c[:, 128:256] = dm
    c[127, 256] = 1.0
    c[15, 384] = 1.0
    for tl in range(128):
        c[tl, 512 + 384 + tl] = 1.0
    return c


def kernel(x, meta_tokens, g_pre_mix, w_in, conv_w_mix, w_proj_conv, w_proj_attn, b_gate, w_out,
           g_post_mix, g_pre_ffn, w_up_gate, conv_w_ffn, w_down, g_post_ffn):
    f32 = np.float32
    x = np.asarray(x, f32)
    B = x.shape[0]
    if "nc" not in _CACHE:
        _CACHE["nc"] = build_program()[0]
    nc = _CACHE["nc"]
    vec = np.zeros((128, NVEC), f32)
    vec[:, 0:8] = np.asarray(g_pre_mix, f32)[0].reshape(8, 128).T
    vec[:, 8:16] = np.asarray(g_pre_ffn, f32)[0].reshape(8, 128).T
    vec[:, 16:40] = np.asarray(conv_w_mix, f32)[0].reshape(3, 8, 128).transpose(2, 1, 0).reshape(128, 24)
    vec[:, 40:106] = np.asarray(conv_w_ffn, f32)[0].reshape(3, NFF, 128).transpose(2, 1, 0).reshape(128, 66)
    vec[:, 106:122] = np.asarray(b_gate, f32)[0].reshape(2, 8, 128).transpose(2, 0, 1).reshape(128, 16)
    gpost = np.stack([np.broadcast_to(np.asarray(g_post_mix, f32)[0][None, :], (128, D)),
                      np.broadcast_to(np.asarray(g_post_ffn, f32)[0][None, :], (128, D))]).astype(f32)
    shared = {
        "meta": np.ascontiguousarray(np.asarray(meta_tokens, f32)),
        "cst": _consts(),
        "vecs": vec,
        "gpost": np.ascontiguousarray(gpost),
        "w_in": np.ascontiguousarray(np.asarray(w_in, f32)[0]),
        "w_pc": np.ascontiguousarray(np.asarray(w_proj_conv, f32)[0]),
        "w_pa": np.ascontiguousarray(np.asarray(w_proj_attn, f32)[0]),
        "w_out": np.ascontiguousarray(np.asarray(w_out, f32)[0]),
        "w_ug": np.ascontiguousarray(np.asarray(w_up_gate, f32)[0]),
        "w_dn": np.ascontiguousarray(np.asarray(w_down, f32)[0]),
    }
    in_maps = []
    for b in range(B):
        m = dict(shared)
        m["x"] = np.ascontiguousarray(x[b])
        in_maps.append(m)
    res = run_bass_kernel_spmd(nc, in_maps, core_ids=list(range(B)))
    return np.stack([np.asarray(r["out"], f32) for r in res.results], axis=0)
```

```python
import bisect
import numpy as np
import concourse.bass as bass
import concourse.mybir as mybir
from concourse.bass_utils import run_bass_kernel_spmd

F32 = mybir.dt.float32
BF16 = mybir.dt.bfloat16
AF = mybir.ActivationFunctionType
ALU = mybir.AluOpType
DSIZE = {F32: 4, BF16: 2}

D = 1024
NM = 16
SEQ = 4096
L = NM + SEQ
DFF = 2816
NFF = DFF // 128
NH = 16
HD = 64
TCH = 512
EPS = 1e-6
N_CORES = 8


class Op:
    __slots__ = ("eng", "fn", "deps", "signal", "chan", "ordinal")

    def __init__(self, eng, fn, chan=None):
        self.eng = eng
        self.fn = fn
        self.deps = []
        self.signal = chan is not None
        self.chan = chan
        self.ordinal = (0, 0)

    @property
    def key(self):
        return self.chan if self.chan is not None else self.eng


class Rec:
    __slots__ = ("lo", "hi", "w", "r")

    def __init__(self, lo, hi, w=None, r=None):
        self.lo = lo
        self.hi = hi
        self.w = w
        self.r = r if r is not None else {}


class Prog:
    ENGS = ("PE", "ACT", "DVE", "POOL", "SP")
    LIMIT = 30000

    def __init__(self, nc):
        self.nc = nc
        self.streams = {e: [] for e in self.ENGS}
        self.spaces = {}
        self.base = {}
        self.chans = []
        self.nops = 0
        self.psum = set()

    def reg(self, t, space=None, base=0, psum=False):
        self.base[t.name] = (space if space is not None else t.name, base)
        if psum:
            self.psum.add(t.name)
        return t

    def rng(self, ap):
        if isinstance(ap, tuple):
            return ap
        space, base = self.base[ap.tensor.name]
        ent = ap.ap
        ds = DSIZE[ap.dtype]
        pstep = ent[0][0]
        off = ap.offset
        if pstep > 0:
            off = off % pstep
        lo = off
        hi = off
        for st, cnt in ent[1:]:
            d = st * (cnt - 1)
            if d < 0:
                lo += d
            else:
                hi += d
        return space, base + lo * ds, base + (hi + 1) * ds

    def _cover(self, space, lo, hi):
        lst = self.spaces.setdefault(space, [])
        i = bisect.bisect_right(lst, lo, key=lambda r: r.hi)
        j = bisect.bisect_left(lst, hi, key=lambda r: r.lo)
        seg = []
        for r in lst[i:j]:
            pieces = [r]
            for cut in (lo, hi):
                nxt = []
                for p in pieces:
                    if p.lo < cut < p.hi:
                        nxt.append(Rec(p.lo, cut, p.w, dict(p.r)))
                        nxt.append(Rec(cut, p.hi, p.w, dict(p.r)))
                    else:
                        nxt.append(p)
                pieces = nxt
            seg.extend(pieces)
        full = []
        cur = lo
        for r in seg:
            if r.hi <= lo or r.lo >= hi:
                full.append(r)
                continue
            if r.lo > cur:
                full.append(Rec(cur, r.lo))
            full.append(r)
            cur = r.hi
        if cur < hi:
            full.append(Rec(cur, hi))
        full.sort(key=lambda x: x.lo)
        lst[i:j] = full
        return [r for r in full if r.lo >= lo and r.hi <= hi]

    def emit(self, eng, fn, reads=(), writes=(), chan=None, extra_deps=()):
        op = Op(eng, fn, chan)
        self.nops += 1
        deps = set(extra_deps)
        reads = list(reads)
        writes = list(writes)
        for ap in reads + writes:
            if isinstance(ap, tuple) or ap.tensor.name not in self.psum:
                continue
            sp, lo, hi = self.rng(ap)
            lo = lo // 2048 * 2048
            hi = (hi + 2047) // 2048 * 2048
            for r in self._cover(sp + "#bank", lo, hi):
                for k_, ro in r.r.items():
                    if k_ != eng:
                        deps.add(ro)
                r.r[eng] = op
        for ap in reads:
            sp, lo, hi = self.rng(ap)
            for r in self._cover(sp, lo, hi):
                if r.w is not None:
                    deps.add(r.w)
                r.r[op.key] = op
        for ap in writes:
            sp, lo, hi = self.rng(ap)
            for r in self._cover(sp, lo, hi):
                if r.w is not None:
                    deps.add(r.w)
                for ro in r.r.values():
                    if ro is not op:
                        deps.add(ro)
                r.w = op
                r.r = {}
        for d in deps:
            if d is op:
                continue
            if d.chan is None and d.eng == eng and eng == "PE":
                continue
            d.signal = True
            op.deps.append(d)
        if chan is not None and chan not in self.chans:
            self.chans.append(chan)
        self.streams[eng].append(op)
        return op

    def finalize(self):
        nc = self.nc
        keys = list(self.ENGS) + self.chans
        sems = {}

        def sem_for(k, ep):
            if (k, ep) not in sems:
                sems[(k, ep)] = nc.alloc_semaphore(name="s_%s_%d" % (k, ep))
            return sems[(k, ep)]

        state = {k: [0, 0] for k in keys}
        finals = {}
        for e in self.ENGS:
            for op in self.streams[e]:
                if op.signal:
                    k = op.key
                    inc = 16 if op.chan is not None else 1
                    ep, c = state[k]
                    if c + inc > self.LIMIT:
                        ep += 1
                        c = 0
                    c += inc
                    state[k] = [ep, c]
                    op.ordinal = (ep, c)
                    if op.chan is not None:
                        finals[(k, ep)] = c
        with nc.Block() as block:
            def run(e, name):
                waited = {}
                for op in self.streams[name]:
                    need = {}
                    for d in op.deps:
                        k = d.key
                        if k not in need or d.ordinal > need[k]:
                            need[k] = d.ordinal
                    for k, (ep, c) in need.items():
                        w = waited.get(k)
                        if w is not None and w >= (ep, c):
                            continue
                        e.wait_ge(sem_for(k, ep), c)
                        waited[k] = (ep, c)
                    ins = op.fn(e)
                    if op.signal:
                        ins.then_inc(sem_for(op.key, op.ordinal[0]), 16 if op.chan is not None else 1)
                if name == "SP":
                    for (k, ep), c in finals.items():
                        e.wait_ge(sem_for(k, ep), c)

            @block.tensor
            def _(e):
                run(e, "PE")

            @block.scalar
            def _(e):
                run(e, "ACT")

            @block.vector
            def _(e):
                run(e, "DVE")

            @block.gpsimd
            def _(e):
                run(e, "POOL")

            @block.sync
            def _(e):
                run(e, "SP")
        return state


CST_COLS = 512 + 896 + 512
NVEC = 8 + 8 + 24 + 66 + 16


class _Stop(Exception):
    pass


STOP_STAGE = [None]
DBG_HEADS = list(range(16))


def build_program(n_chunks_limit=None):
    nc = bass.Bass("TRN2", target_bir_lowering=False)
    P = Prog(nc)

    def din(name, shape, dt=F32):
        return nc.dram_tensor(name, shape, dt, kind="ExternalInput").ap()

    x_d = din("x", [SEQ, D])
    meta_d = din("meta", [NM, D])
    cst_d = din("cst", [128, CST_COLS])
    vec_d = din("vecs", [128, NVEC])
    gpost_d = din("gpost", [2, 128, D])
    w_in_d = din("w_in", [D, 8 * D])
    w_pc_d = din("w_pc", [D, D])
    w_pa_d = din("w_pa", [D, D])
    w_out_d = din("w_out", [D, D])
    w_ug_d = din("w_ug", [D, 2 * DFF])
    w_dn_d = din("w_dn", [DFF, D])
    out_d = nc.dram_tensor("out", [SEQ, D], F32, kind="ExternalOutput").ap()

    win_s = nc.dram_tensor("win_s", [D, 8 * D], BF16, kind="Internal").ap()
    wmix_s = nc.dram_tensor("wmix_s", [D, 4 * D], BF16, kind="Internal").ap()
    wout_s = nc.dram_tensor("wout_s", [D, D], BF16, kind="Internal").ap()
    wug_s = nc.dram_tensor("wug_s", [D, 2 * DFF], BF16, kind="Internal").ap()
    wdn_s = nc.dram_tensor("wdn_s", [DFF, D], BF16, kind="Internal").ap()

    SB0 = 16512
    cursor = [SB0]

    def sb(name, shape, dt, off=None, space=None):
        n = DSIZE[dt]
        for s_ in shape[1:]:
            n *= s_
        if off is None:
            a = cursor[0]
            cursor[0] += (n + 31) // 32 * 32
            assert cursor[0] <= 229344, ("sbuf overflow", name, cursor[0])
            t = nc.alloc_sbuf_tensor_at(name, shape, dt, offset=a)
            P.reg(t, "SB", a)
        else:
            t = nc.alloc_sbuf_tensor_at(name, shape, dt, offset=off)
            P.reg(t, "SB", off)
        return t

    Kc = sb("Kc", [128, 8, L], BF16)
    dVc = sb("dVc", [128, 33, D], BF16)
    cstb = sb("cstb", [128, CST_COLS], BF16)
    vecs = sb("vecs_sb", [128, NVEC], F32)
    vprev = sb("vprev", [128, D], BF16)
    uhm = sb("uhm", [128, 8, 2], F32)
    uhf = sb("uhf", [128, NFF, 2], F32)
    stat = sb("stat", [128, 96], F32)
    Wb = [sb("W0", [128, 8, 512], BF16), sb("W1", [128, 8, 512], BF16)]
    A0 = cursor[0]
    ARENA = 55296
    assert A0 + ARENA <= 229344, (A0, ARENA)
    KB = 1024

    def ar(name, shape, dt, off):
        n = DSIZE[dt]
        for s_ in shape[1:]:
            n *= s_
        assert off + n <= ARENA, (name, off, n)
        return sb(name, shape, dt, off=A0 + off)

    xntok = ar("xntok", [128, 4, D], BF16, 0)
    ycT = ar("ycT", [128, 8, TCH], BF16, 0)
    xn2tok = ar("xn2tok", [128, 4, D], BF16, 0)
    actT = ar("actT", [128, NFF, TCH], BF16, 0)
    Vtok = ar("Vtok", [128, 4, D], BF16, 8192)
    mergedT = ar("mergedT", [128, 8, TCH], BF16, 8192)
    Fb = [ar("F%d" % i, [128, 512], F32, 16384 + i * 2048) for i in range(3)]
    Otok = ar("Otok", [128, D], BF16, 16384)
    junk = ar("junk", [128, D], BF16, 20480)
    gbc1 = ar("gbc1", [128, D], F32, 16384)
    xnT = ar("xnT", [128, 8, TCH], BF16, 22528)
    xn2T = ar("xn2T", [128, 8, TCH], BF16, 22528)
    tmp4 = ar("tmp4", [128, D], F32, 22528)
    gbc2 = ar("gbc2", [128, D], F32, 26624)
    xin = ar("xin", [128, 4, D], F32, 30720)
    OT = ar("OT", [128, 8, TCH], BF16, 30720)
    QT = ar("QT", [128, 8, TCH], BF16, 38912)
    Ares = ar("Ares", [128, 4, D], F32, 30720)
    Pb = [ar("P%d" % i, [128, 512], BF16, 47104 + i * KB) for i in range(4)]
    PTb = [ar("PT%d" % i, [128, 512], BF16, 51200 + i * KB) for i in range(2)]
    ub = ar("ub", [128, 516], F32, 38912)
    hs = ar("hs", [128, 512], F32, 38912 + 2112)
    acc = ar("acc", [128, 512], F32, 38912 + 2112 + 2048)
    sg1 = ar("sg1", [128, 512], F32, 38912 + 2112 + 4096)
    sg2 = ar("sg2", [128, 512], F32, 38912 + 2112 + 6144)
    m1 = ar("m1", [128, 512], F32, 38912 + 2112)
    ubf = ar("ubf", [128, 516], F32, 47104)
    accf = ar("accf", [128, 512], F32, 47104 + 2080)
    gef = ar("gef", [128, 512], F32, 47104 + 2080 + 2048)
    NST = 4
    ST32 = [sb("ST32_%d" % i, [128, 3072], F32, off=SB0 + i * 12 * KB) for i in range(NST)]
    STB = [sb("STB_%d" % i, [128, 3072], BF16, off=SB0 + 65792 + i * 6 * KB) for i in range(NST)]
    cst32 = ar("cst32", [128, CST_COLS], F32, 36 * KB)

    psA = P.reg(nc.alloc_psum_tensor("psA", [128, 8, 512], F32), psum=True)
    psG = psA[:, 0:2, :]
    psZ = psA[:, 2:4, :]
    psO = psA[:, 4:6, :].rearrange("p a b -> p (a b)")
    psT = psA[:, 6:8, :].bitcast(BF16)

    def gb(i, r0, r1, c0, c1):
        i = i % 4
        return psG[r0:r1, i, c0:c1] if i < 2 else psZ[r0:r1, i - 2, c0:c1]

    def bk6(i, r0, r1, c0, c1):
        i = i % 6
        if i < 2:
            return psG[r0:r1, i, c0:c1]
        if i < 4:
            return psZ[r0:r1, i - 2, c0:c1]
        return psO[r0:r1, (i - 4) * 512 + c0:(i - 4) * 512 + c1]

    rot = [0]

    ident = cstb[:, 0:128]
    Dmat = cstb[:, 128:256]
    Emat = cstb[:, 256:384]
    E16 = cstb[:, 384:512]
    IND0 = 512
    ZER0 = 512 + 896

    def v_gpm(ft):
        return vecs[:, ft:ft + 1]

    def v_gpf(ft):
        return vecs[:, 8 + ft:9 + ft]

    def v_cwm(ft, i):
        return vecs[:, 16 + ft * 3 + i:17 + ft * 3 + i]

    def v_cwf(ft, i):
        return vecs[:, 40 + ft * 3 + i:41 + ft * 3 + i]

    def v_bg(br, ft):
        return vecs[:, 106 + br * 8 + ft:107 + br * 8 + ft]

    def mm(out, lhsT, rhs, start, stop):
        P.emit("PE", lambda e, o=out, l=lhsT, r=rhs, s=start, t=stop: e.matmul(o, lhsT=l, rhs=r, start=s, stop=t),
               reads=[lhsT, rhs], writes=[out])

    def tr(out, in_, idn):
        P.emit("PE", lambda e, o=out, i=in_, d=idn: e.transpose(o, i, d), reads=[in_, idn], writes=[out])

    def act(out, in_, func, bias=None, scale=1.0, accum=None, extra_reads=()):
        rd = [in_] + list(extra_reads)
        wr = [out]
        kw = {}
        if bias is not None:
            kw["bias"] = bias
            if not isinstance(bias, float):
                rd.append(bias)
        if accum is not None:
            kw["accum_out"] = accum
            wr.append(accum)
        if not isinstance(scale, float):
            rd.append(scale)
        P.emit("ACT", lambda e, o=out, i=in_, f=func, s=scale, k=kw: e.activation(out=o, in_=i, func=f, scale=s, **k),
               reads=rd, writes=wr)

    def ts(eng, out, in0, s1, s2, op0, op1=None):
        rd = [in0]
        if not isinstance(s1, float):
            rd.append(s1)
        if s2 is not None and not isinstance(s2, float):
            rd.append(s2)
        if op1 is None:
            P.emit(eng, lambda e, o=out, i=in0, a=s1, p0=op0: e.tensor_scalar(o, i, a, None, op0=p0), reads=rd, writes=[out])
        else:
            P.emit(eng, lambda e, o=out, i=in0, a=s1, b=s2, p0=op0, p1=op1: e.tensor_scalar(o, i, a, b, op0=p0, op1=p1),
                   reads=rd, writes=[out])

    def tt(eng, out, in0, in1, op):
        P.emit(eng, lambda e, o=out, a=in0, b=in1, p=op: e.tensor_tensor(out=o, in0=a, in1=b, op=p), reads=[in0, in1], writes=[out])

    def stt(eng, out, in0, scalar, in1, op0, op1):
        rd = [in0, in1]
        if not isinstance(scalar, float):
            rd.append(scalar)
        P.emit(eng, lambda e, o=out, a=in0, s=scalar, b=in1, p0=op0, p1=op1: e.scalar_tensor_tensor(out=o, in0=a, scalar=s, in1=b, op0=p0, op1=p1),
               reads=rd, writes=[out])

    def cp(eng, out, in_):
        P.emit(eng, lambda e, o=out, i=in_: e.tensor_copy(o, i), reads=[in_], writes=[out])

    def memset(eng, out, val):
        P.emit(eng, lambda e, o=out, v=val: e.memset(o, v), writes=[out])

    def dma(out, in_, chan, reads=(), writes=(), extra=()):
        return P.emit("SP", lambda e, o=out, i=in_: e.dma_start(out=o, in_=i), reads=reads, writes=writes, chan=chan, extra_deps=extra)

    dma(cst32[:, :], cst_d, "ldc0", writes=[cst32[:, :]])
    dma(vecs[:, :], vec_d, "ldc1", writes=[vecs[:, :]])
    cp("DVE", cstb[:, :], cst32[:, :])
    memset("DVE", uhm[:, :, :], 0.0)
    memset("DVE", uhf[:, :, :], 0.0)

    pieces = []

    for rg in range(8):
        r0 = rg * 128
        pieces.append(dict(loads=[(0, 3072, w_in_d[r0:r0 + 128, 0:3072])], n=3072, perm=(3, 8), dst=win_s[r0:r0 + 128, 0:3072], name="win_s"))
        pieces.append(dict(loads=[(0, 2560, w_in_d[r0:r0 + 128, 3072:5632])], n=2560, perm=None, dst=win_s[r0:r0 + 128, 3072:5632], name="win_s"))
        pieces.append(dict(loads=[(0, 2560, w_in_d[r0:r0 + 128, 5632:8192])], n=2560, perm=None, dst=win_s[r0:r0 + 128, 5632:8192], name="win_s"))
    for rg in range(8):
        r0 = rg * 128
        pieces.append(dict(loads=[(0, 1024, w_pc_d[r0:r0 + 128, :]), (1024, 1024, w_in_d[r0:r0 + 128, 6144:7168]),
                                  (2048, 1024, w_pa_d[r0:r0 + 128, :]), (3072, 1024, w_in_d[r0:r0 + 128, 7168:8192])],
                           n=4096, perm=(4, 8), dst=wmix_s[r0:r0 + 128, :], name="wmix_s", big=True))
        pieces.append(dict(loads=[(0, 1024, w_out_d[r0:r0 + 128, :])], n=1024, perm=None, dst=wout_s[r0:r0 + 128, :], name="wout_s"))
    for rg in range(8):
        r0 = rg * 128
        for hf in range(2):
            c0 = hf * 1408
            pieces.append(dict(loads=[(0, 1408, w_ug_d[r0:r0 + 128, c0:c0 + 1408]), (1408, 1408, w_ug_d[r0:r0 + 128, DFF + c0:DFF + c0 + 1408])],
                               n=2816, perm=(2, 11), dst=wug_s[r0:r0 + 128, hf * 2816:(hf + 1) * 2816], name="wug_s"))
    for rg in range(NFF):
        r0 = rg * 128
        pieces.append(dict(loads=[(0, 1024, w_dn_d[r0:r0 + 128, :])], n=1024, perm=None, dst=wdn_s[r0:r0 + 128, :], name="wdn_s"))

    scratch_count = {}
    cast_engs = ["ACT", "DVE", "POOL"]
    flat = []
    for pc in pieces:
        if pc["n"] > 3072:
            T_, F_ = pc["perm"]
            for hf in range(2):
                loads = [(t * 512, 512, src[:, hf * 512:(hf + 1) * 512]) for t, (_, _, src) in enumerate(pc["loads"])]
                flat.append(dict(loads=loads, n=2048, perm=(T_, F_ // 2), dst=pc["dst"][:, hf * 2048:(hf + 1) * 2048], name=pc["name"]))
        else:
            flat.append(pc)

    def prep_load(i):
        sp_ = flat[i]
        b = i % NST
        for (o, w, src) in sp_["loads"]:
            dma(ST32[b][:, o:o + w], src, "pl%d" % b, writes=[ST32[b][:, o:o + w]])

    def prep_cast_store(i):
        sp_ = flat[i]
        b = i % NST
        n = sp_["n"]
        eng = cast_engs[i % 3]
        if sp_["perm"] is None:
            cin = ST32[b][:, 0:n]
            cout = STB[b][:, 0:n]
        else:
            T_, F_ = sp_["perm"]
            cin = ST32[b][:, 0:n].rearrange("p (t f c) -> p f t c", t=T_, f=F_, c=128)
            cout = STB[b][:, 0:n].rearrange("p (f t c) -> p f t c", f=F_, t=T_, c=128)
        if eng == "ACT":
            P.emit("ACT", lambda e, o=cout, i_=cin: e.activation(out=o, in_=i_, func=AF.Copy), reads=[ST32[b][:, 0:n]], writes=[STB[b][:, 0:n]])
        else:
            P.emit(eng, lambda e, o=cout, i_=cin: e.tensor_copy(o, i_), reads=[ST32[b][:, 0:n]], writes=[STB[b][:, 0:n]])
        k = scratch_count.get(sp_["name"], 0)
        scratch_count[sp_["name"]] = k + 1
        dma(sp_["dst"], STB[b][:, 0:n], "ps%d" % b, reads=[STB[b][:, 0:n]], writes=[(sp_["name"], k, k + 1)])

    LOOKAHEAD = NST - 1
    for i in range(min(LOOKAHEAD, len(flat))):
        prep_load(i)
    for i in range(len(flat)):
        if i + LOOKAHEAD < len(flat):
            prep_load(i + LOOKAHEAD)
        prep_cast_store(i)

    def scr_read(name):
        return (name, 0, scratch_count[name])

    wstate = {"n": 0}

    def wtile(src_ap, nk, ncols, name):
        b = wstate["n"] % 2
        wstate["n"] += 1
        dst = Wb[b][:, 0:nk, 0:ncols]
        dma(dst, src_ap.rearrange("(k p) c -> p k c", p=128), "w%d" % b, reads=[scr_read(name)], writes=[dst])
        return Wb[b]

    def rms_rstd(src_list, nrows, col0):
        n = len(src_list)
        memset("DVE", stat[0:nrows, col0:col0 + n], 0.0)
        for i, s_ in enumerate(src_list):
            act(junk[0:nrows, :], s_, AF.Square, accum=stat[0:nrows, col0 + i:col0 + i + 1])
        ts("DVE", stat[0:nrows, col0 + 8:col0 + 8 + n], stat[0:nrows, col0:col0 + n], 1.0 / D, EPS, ALU.mult, ALU.add)
        act(stat[0:nrows, col0 + 8:col0 + 8 + n], stat[0:nrows, col0 + 8:col0 + 8 + n], AF.Ln)
        act(stat[0:nrows, col0 + 16:col0 + 16 + n], stat[0:nrows, col0 + 8:col0 + 8 + n], AF.Exp, scale=-0.5)
        return [stat[0:nrows, col0 + 16 + i:col0 + 17 + i] for i in range(n)]

    def transpose_to_feature_major(src_tok, dstT, nb, BT, gain_fn):
        T = nb * BT
        for ft in range(8):
            par = ft % 2
            for j in range(nb):
                tr(psT[:, par, j * BT:(j + 1) * BT], src_tok[0:BT, j, ft * 128:(ft + 1) * 128], ident[0:BT, 0:BT])
            if gain_fn is None:
                cp("DVE", dstT[:, ft, 0:T], psT[:, par, 0:T])
            else:
                ts("DVE", dstT[:, ft, 0:T], psT[:, par, 0:T], gain_fn(ft), None, ALU.mult)

    def chunk(c):
        if c == 0:
            pos0, T, nb, BT = 0, NM, 1, NM
        else:
            pos0, T, nb, BT = NM + TCH * (c - 1), TCH, 4, 128
        blk0 = 0 if c == 0 else 1 + 4 * (c - 1)

        def xrows(dst):
            if c == 0:
                dma(dst[0:NM, 0, :], meta_d, "ldx0", writes=[dst[0:NM, 0, :]])
            else:
                r0 = pos0 - NM
                for jx in range(nb):
                    dma(dst[:, jx, :], x_d[r0 + jx * 128:r0 + (jx + 1) * 128, :], "ldx%d" % jx, writes=[dst[:, jx, :]])

        xrows(xin)
        rs = rms_rstd([xin[0:BT, j, :] for j in range(nb)], BT, 0)
        for j in range(nb):
            ts("DVE", xntok[0:BT, j, :], xin[0:BT, j, :], rs[j], None, ALU.mult)
        transpose_to_feature_major(xntok, xnT, nb, BT, v_gpm)

        if STOP_STAGE[0] == 3:
            raise _Stop()
        gi = 0
        for half in range(2):
            w = wtile(win_s[:, 4096 + half * 512:4096 + (half + 1) * 512], 8, 512, "win_s")
            for f4 in range(4):
                pr = half * 4 + f4
                pz = gb(gi, 0, 128, 0, T)
                gi += 1
                for kc in range(8):
                    mm(pz, w[:, kc, f4 * 128:(f4 + 1) * 128], xnT[:, kc, 0:T], kc == 0, kc == 7)
                act(Kc[:, pr, pos0:pos0 + T], pz, AF.Copy)
        for half in range(2):
            w = wtile(win_s[:, 5120 + half * 512:5120 + (half + 1) * 512], 8, 512, "win_s")
            for j in range(nb):
                pz = gb(gi, 0, BT, 0, 512)
                gi += 1
                for kc in range(8):
                    mm(pz, xnT[:, kc, j * BT:(j + 1) * BT], w[:, kc, :], kc == 0, kc == 7)
                cp("DVE", Vtok[0:BT, j, half * 512:(half + 1) * 512], pz)
        for j in range(nb):
            for half in range(2):
                pz = gb(gi, 0, BT, 0, 512)
                gi += 1
                hs_ = slice(half * 512, (half + 1) * 512)
                has_prev = not (c == 0)
                mm(pz, Dmat[0:BT, 0:BT], Vtok[0:BT, j, hs_], True, not has_prev)
                if has_prev:
                    if j > 0:
                        mm(pz, Emat[0:128, 0:BT], Vtok[0:128, j - 1, hs_], False, True)
                    elif c == 1:
                        mm(pz, E16[0:NM, 0:BT], vprev[0:NM, hs_], False, True)
                    else:
                        mm(pz, Emat[0:128, 0:BT], vprev[0:128, hs_], False, True)
                cp("DVE", dVc[0:BT, blk0 + j, hs_], pz)
        for half in range(2):
            w = wtile(win_s[:, 3072 + half * 512:3072 + (half + 1) * 512], 8, 512, "win_s")
            for f4 in range(4):
                pr = half * 4 + f4
                pz = gb(gi, 0, 128, 0, T)
                gi += 1
                for kc in range(8):
                    mm(pz, w[:, kc, f4 * 128:(f4 + 1) * 128], xnT[:, kc, 0:T], kc == 0, kc == 7)
                act(QT[:, pr, 0:T], pz, AF.Identity, scale=0.125)

        if STOP_STAGE[0] == 4:
            raise _Stop()
        for j in range(nb):
            q0 = j * BT
            kchunks = [(pos0, BT * (j + 1), True)]
            for cc in range(c - 1, 0, -1):
                kchunks.append((NM + TCH * (cc - 1), TCH, False))
            if c >= 1:
                kchunks.append((0, NM, False))
            nk_ = len(kchunks)
            tiles = []
            for hh in range(8):
                hb = 8 + (hh ^ 1)
                for ki in range(nk_):
                    tiles.append((hh, ki))
                    tiles.append((hb, ki))
            N = len(tiles)
            for half in range(2):
                mm(psO[0:BT, half * 512:(half + 1) * 512], ident[0:BT, 0:BT], Vtok[0:BT, j, half * 512:(half + 1) * 512], True, False)

            def zb(n, Wk):
                return gb(n, 0, BT, 0, Wk)

            def st_qk(n):
                h, ki = tiles[n]
                k0, Wk, dg = kchunks[ki]
                pr, hp = h // 2, (h % 2) * 64
                mm(zb(n, Wk), QT[hp:hp + 64, pr, q0:q0 + BT], Kc[hp:hp + 64, pr, k0:k0 + Wk], True, True)

            def st_sig(n):
                h, ki = tiles[n]
                k0, Wk, dg = kchunks[ki]
                act(Fb[n % 3][0:BT, 0:Wk], zb(n, Wk), AF.Sigmoid, scale=-1.0)

            def st_scan(n):
                h, ki = tiles[n]
                k0, Wk, dg = kchunks[ki]
                fb = Fb[n % 3]
                pb = Pb[n % 4]
                if dg:
                    i0 = IND0 + 384 - 128 * j
                    d1 = cstb[0:BT, i0:i0 + Wk]
                    init = 0.0
                    rd = [fb[0:BT, 0:Wk], d1]
                else:
                    d1 = cstb[0:BT, ZER0:ZER0 + Wk]
                    init = Pb[(n - 2) % 4][0:BT, 0:1]
                    rd = [fb[0:BT, 0:Wk], d1, init]
                P.emit("DVE", lambda e, o=pb[0:BT, Wk - 1::-1] if Wk < 512 else pb[0:BT, ::-1],
                       a=fb[0:BT, Wk - 1::-1] if Wk < 512 else fb[0:BT, ::-1],
                       b=d1[:, ::-1], i=init: e.tensor_tensor_scan(out=o, data0=a, data1=b, initial=i, op0=ALU.mult, op1=ALU.add),
                       reads=rd, writes=[pb[0:BT, 0:Wk]])

            def subblocks(Wk):
                return [(s0, min(128, Wk - s0)) for s0 in range(0, Wk, 128)]

            def st_tr(n):
                h, ki = tiles[n]
                k0, Wk, dg = kchunks[ki]
                pb = Pb[n % 4]
                for sbi, (s0, ws) in enumerate(subblocks(Wk)):
                    tr(psT[0:ws, n % 2, sbi * 128:sbi * 128 + BT], pb[0:BT, s0:s0 + ws], ident[0:BT, 0:BT])

            def st_ev(n):
                h, ki = tiles[n]
                k0, Wk, dg = kchunks[ki]
                nsb = len(subblocks(Wk))
                ws_max = min(128, Wk)
                src = psT[0:ws_max, n % 2, 0:(nsb - 1) * 128 + BT]
                dst = PTb[n % 2][0:ws_max, 0:(nsb - 1) * 128 + BT]
                act(dst, src, AF.Copy)

            def st_pv(n):
                h, ki = tiles[n]
                k0, Wk, dg = kchunks[ki]
                sbs = subblocks(Wk)
                for sbi, (s0, ws) in enumerate(sbs):
                    pos = k0 + s0
                    blk = 0 if pos < NM else 1 + (pos - NM) // 128
                    last = (n >= N - 2 and sbi == len(sbs) - 1)
                    mm(psO[0:BT, h * HD:(h + 1) * HD], PTb[n % 2][0:ws, sbi * 128:sbi * 128 + BT], dVc[0:ws, blk, h * HD:(h + 1) * HD], False, last)

            NP_ = N // 2
            for m in range(NP_ + 3):
                for n in (2 * m, 2 * m + 1):
                    if n < N:
                        st_qk(n)
                for n in (2 * m - 2, 2 * m - 1):
                    if 0 <= n < N:
                        st_sig(n)
                for n in (2 * m - 2, 2 * m - 1):
                    if 0 <= n < N:
                        st_scan(n)
                for n in (2 * m - 6, 2 * m - 5):
                    if 0 <= n < N:
                        st_pv(n)
                for n in (2 * m - 4, 2 * m - 3):
                    if 0 <= n < N:
                        st_tr(n)
                        st_ev(n)
            act(Otok[0:BT, :], psO[0:BT, :], AF.Copy)
            for g4 in range(2):
                for f4 in range(4):
                    ft = g4 * 4 + f4
                    tr(psT[:, g4, f4 * 128:f4 * 128 + BT], Otok[0:BT, ft * 128:(ft + 1) * 128], ident[0:BT, 0:BT])
                act(OT[:, g4 * 4:g4 * 4 + 4, q0:q0 + BT], psT[:, g4, 0:512].rearrange("p (f t) -> p f t", f=4)[:, :, 0:BT], AF.Copy)
        cp("DVE", vprev[0:BT, :], Vtok[0:BT, nb - 1, :])

        if STOP_STAGE[0] == 5:
            raise _Stop()
        for ft in range(8):
            w = wtile(win_s[:, ft * 384:(ft + 1) * 384], 8, 384, "win_s")
            pb_, pc_, ph_ = [bk6(rot[0] + i_, 0, 128, 0, T) for i_ in range(3)]
            rot[0] += 3
            for oi, pz in enumerate((pb_, pc_, ph_)):
                for kc in range(8):
                    mm(pz, w[:, kc, oi * 128:(oi + 1) * 128], xnT[:, kc, 0:T], kc == 0, kc == 7)
            act(hs[:, 0:T], ph_, AF.Copy)
            cp("DVE", ub[:, 0:2], uhm[:, ft, :])
            tt("DVE", ub[:, 2:2 + T], pc_, hs[:, 0:T], ALU.mult)
            cp("DVE", uhm[:, ft, :], ub[:, T:T + 2])
            ts("DVE", acc[:, 0:T], ub[:, 2:2 + T], v_cwm(ft, 2), None, ALU.mult)
            stt("DVE", acc[:, 0:T], ub[:, 1:1 + T], v_cwm(ft, 1), acc[:, 0:T], ALU.mult, ALU.add)
            stt("DVE", acc[:, 0:T], ub[:, 0:T], v_cwm(ft, 0), acc[:, 0:T], ALU.mult, ALU.add)
            tt("DVE", ycT[:, ft, 0:T], pb_, acc[:, 0:T], ALU.mult)

        if STOP_STAGE[0] == 6:
            raise _Stop()
        for ft in range(8):
            w = wtile(wmix_s[:, ft * 512:(ft + 1) * 512], 8, 512, "wmix_s")
            p_bc, p_gc, p_ba, p_ga = [bk6(rot[0] + i_, 0, 128, 0, T) for i_ in range(4)]
            rot[0] += 4
            for oi, (pz, rhsT) in enumerate(((p_bc, ycT), (p_gc, xnT), (p_ba, OT), (p_ga, xnT))):
                for kc in range(8):
                    mm(pz, w[:, kc, oi * 128:(oi + 1) * 128], rhsT[:, kc, 0:T], kc == 0, kc == 7)
            act(sg1[:, 0:T], p_gc, AF.Sigmoid, bias=v_bg(0, ft))
            act(sg2[:, 0:T], p_ga, AF.Sigmoid, bias=v_bg(1, ft))
            tt("DVE", m1[:, 0:T], p_bc, sg1[:, 0:T], ALU.mult)
            tt("DVE", sg2[:, 0:T], p_ba, sg2[:, 0:T], ALU.mult)
            tt("DVE", mergedT[:, ft, 0:T], m1[:, 0:T], sg2[:, 0:T], ALU.add)

        if STOP_STAGE[0] == 7:
            raise _Stop()
        wo = [wtile(wout_s[:, half * 512:(half + 1) * 512], 8, 512, "wout_s") for half in range(2)]
        xrows(Ares)
        dma(gbc1[:, :], gpost_d[0], "ldg1", writes=[gbc1[:, :]])
        pms = [psA[0:BT, 2 * j:2 * j + 2, :].rearrange("p a b -> p (a b)") for j in range(nb)]
        for j in range(nb):
            for half in range(2):
                for kc in range(8):
                    mm(pms[j][:, half * 512:(half + 1) * 512], mergedT[:, kc, j * BT:(j + 1) * BT], wo[half][:, kc, :], kc == 0, kc == 7)
        r1s = rms_rstd(pms, BT, 24)
        for j in range(nb):
            stt("DVE", tmp4[0:BT, :], pms[j], r1s[j], gbc1[0:BT, :], ALU.mult, ALU.mult)
            tt("DVE", Ares[0:BT, j, :], tmp4[0:BT, :], Ares[0:BT, j, :], ALU.add)
        r2s = rms_rstd([Ares[0:BT, j, :] for j in range(nb)], BT, 28)
        for j in range(nb):
            ts("DVE", xn2tok[0:BT, j, :], Ares[0:BT, j, :], r2s[j], None, ALU.mult)
        transpose_to_feature_major(xn2tok, xn2T, nb, BT, v_gpf)

        if STOP_STAGE[0] == 8:
            raise _Stop()
        Th = T
        for ti in range(NFF // 2):
            w = wtile(wug_s[:, ti * 512:(ti + 1) * 512], 8, 512, "wug_s")
            for fi in range(2):
                ft = ti * 2 + fi
                pu, pg = gb(rot[0], 0, 128, 0, Th), gb(rot[0] + 1, 0, 128, 0, Th)
                rot[0] += 2
                for oi, pz in enumerate((pu, pg)):
                    for kc in range(8):
                        mm(pz, w[:, kc, (fi * 2 + oi) * 128:(fi * 2 + oi + 1) * 128], xn2T[:, kc, 0:Th], kc == 0, kc == 7)
                cp("DVE", ubf[:, 0:2], uhf[:, ft, :])
                act(ubf[:, 2:2 + Th], pu, AF.Copy)
                cp("DVE", uhf[:, ft, :], ubf[:, Th:Th + 2])
                act(accf[:, 0:Th], pu, AF.Identity, scale=v_cwf(ft, 2))
                stt("DVE", accf[:, 0:Th], ubf[:, 1:1 + Th], v_cwf(ft, 1), accf[:, 0:Th], ALU.mult, ALU.add)
                stt("DVE", accf[:, 0:Th], ubf[:, 0:Th], v_cwf(ft, 0), accf[:, 0:Th], ALU.mult, ALU.add)
                act(gef[:, 0:Th], accf[:, 0:Th], AF.Gelu_apprx_tanh)
                tt("DVE", actT[:, ft, 0:Th], pg, gef[:, 0:Th], ALU.mult)
        if c == 0:
            return
        dma(gbc2[:, :], gpost_d[1], "ldg2", writes=[gbc2[:, :]])
        for kg in range(3):
            nk = 8 if kg < 2 else NFF - 16
            for colh in range(2):
                w = wtile(wdn_s[kg * 1024:kg * 1024 + nk * 128, colh * 512:(colh + 1) * 512], nk, 512, "wdn_s")
                for jj in range(nb):
                    po = psA[0:BT, 2 * jj + colh, :]
                    for kk in range(nk):
                        mm(po, actT[:, kg * 8 + kk, jj * BT:(jj + 1) * BT], w[:, kk, :], kg == 0 and kk == 0, kg == 2 and kk == nk - 1)
        pds = [psA[0:BT, 2 * jj:2 * jj + 2, :].rearrange("p a b -> p (a b)") for jj in range(nb)]
        r3s = rms_rstd(pds, BT, 48)
        for jj in range(nb):
            pm = pds[jj]
            r3 = r3s[jj]
            stt("DVE", tmp4[0:BT, :], pm, r3, gbc2[0:BT, :], ALU.mult, ALU.mult)
            tt("DVE", tmp4[0:BT, :], tmp4[0:BT, :], Ares[0:BT, jj, :], ALU.add)
            r0 = pos0 - NM + jj * 128
            dma(out_d[r0:r0 + 128, :], tmp4[0:BT, :], "sto", reads=[tmp4[0:BT, :]])

    nch = 9 if n_chunks_limit is None else n_chunks_limit
    try:
        for c in range(nch):
            chunk(c)
    except _Stop:
        pass
    P.finalize()
    return nc, P


_CACHE = {}


def _consts():
    c = np.zeros((128, CST_COLS), np.float32)
    c[:, 0:128] = np.eye(128, dtype=np.float32)
    dm = -np.eye(128, dtype=np.float32)
    for s in range(1, 128):
        dm[s - 1, s] = 1.0
    c[:, 128:256] = dm
    c[127, 256] = 1.0
    c[15, 384] = 1.0
    for tl in range(128):
        c[tl, 512 + 384 + tl] = 1.0
    return c


def kernel(x, meta_tokens, g_pre_mix, w_in, conv_w_mix, w_proj_conv, w_proj_attn, b_gate, w_out,
           g_post_mix, g_pre_ffn, w_up_gate, conv_w_ffn, w_down, g_post_ffn):
    f32 = np.float32
    x = np.asarray(x, f32)
    B = x.shape[0]
    if "nc" not in _CACHE:
        _CACHE["nc"] = build_program()[0]
    nc = _CACHE["nc"]
    vec = np.zeros((128, NVEC), f32)
    vec[:, 0:8] = np.asarray(g_pre_mix, f32)[0].reshape(8, 128).T
    vec[:, 8:16] = np.asarray(g_pre_ffn, f32)[0].reshape(8, 128).T
    vec[:, 16:40] = np.asarray(conv_w_mix, f32)[0].reshape(3, 8, 128).transpose(2, 1, 0).reshape(128, 24)
    vec[:, 40:106] = np.asarray(conv_w_ffn, f32)[0].reshape(3, NFF, 128).transpose(2, 1, 0).reshape(128, 66)
    vec[:, 106:122] = np.asarray(b_gate, f32)[0].reshape(2, 8, 128).transpose(2, 0, 1).reshape(128, 16)
    gpost = np.stack([np.broadcast_to(np.asarray(g_post_mix, f32)[0][None, :], (128, D)),
                      np.broadcast_to(np.asarray(g_post_ffn, f32)[0][None, :], (128, D))]).astype(f32)
    shared = {
        "meta": np.ascontiguousarray(np.asarray(meta_tokens, f32)),
        "cst": _consts(),
        "vecs": vec,
        "gpost": np.ascontiguousarray(gpost),
        "w_in": np.ascontiguousarray(np.asarray(w_in, f32)[0]),
        "w_pc": np.ascontiguousarray(np.asarray(w_proj_conv, f32)[0]),
        "w_pa": np.ascontiguousarray(np.asarray(w_proj_attn, f32)[0]),
        "w_out": np.ascontiguousarray(np.asarray(w_out, f32)[0]),
        "w_ug": np.ascontiguousarray(np.asarray(w_up_gate, f32)[0]),
        "w_dn": np.ascontiguousarray(np.asarray(w_down, f32)[0]),
    }
    in_maps = []
    for b in range(B):
        m = dict(shared)
        m["x"] = np.ascontiguousarray(x[b])
        in_maps.append(m)
    res = run_bass_kernel_spmd(nc, in_maps, core_ids=list(range(B)))
    return np.stack([np.asarray(r["out"], f32) for r in res.results], axis=0)
```
